# Optimizing a Trainium2 kernel written in Bass

```python
import jax, jax.numpy as jnp
from jax import lax
import numpy as np

D_MODEL = 2048
BATCH = 1
SEQ = 8192
DEPTH = 4

N_MEM = 256
D_MIX = 2 * D_MODEL
D_GROUP = D_MIX // 4
RW_HD = 64
RW_HEADS = D_GROUP // RW_HD
RW_DECAY_LORA = 64
RW_AAA_LORA = 64
RW_COLS = 3 * D_GROUP + RW_DECAY_LORA + RW_AAA_LORA
RW_GN_EPS = 64e-5
FX_HD = 64
FX_HEADS = D_GROUP // FX_HD
FX_BLOCK = 128
FX_COLS = 3 * D_GROUP + FX_HEADS
SSM_HD = 64
SSM_HEADS = D_GROUP // SSM_HD
SSM_GROUPS = 4
SSM_STATE = 128
SSM_CONV = 4
SSM_CHUNK = 128
SSM_CONV_DIM = D_GROUP + 2 * SSM_GROUPS * SSM_STATE
SSM_COLS = SSM_CONV_DIM + SSM_HEADS
MEM_HEADS = 4
MEM_HD = D_GROUP // MEM_HEADS

NORM_EPS = 1e-6
IN_SPLITS = (D_MIX, RW_COLS, FX_COLS, SSM_COLS, D_GROUP)
D_IN = sum(IN_SPLITS)
F32 = jnp.float32

kernel_name = "hybrid_rwkv7_fox_mamba2_memxattn"


def _split(t, sizes):
    return jnp.split(t, np.cumsum(sizes)[:-1].tolist(), axis=-1)


def _rmsnorm(t, w):
    tf = t.astype(F32)
    return tf * lax.rsqrt(jnp.mean(tf * tf, axis=-1, keepdims=True) + NORM_EPS) * w.astype(F32)


def _rwkv7_group(p, g, mu, w0, w_up, a0, a_up, k_k, k_a, r_k, lnx_w, lnx_b):
    bsz, seq = p.shape[0], p.shape[1]
    p = p.astype(F32)
    p_prev = jnp.pad(p, ((0, 0), (1, 0), (0, 0)))[:, :-1]
    p = p + (p_prev - p) * mu
    r, k, v, wd, ad = _split(p, (D_GROUP, D_GROUP, D_GROUP, RW_DECAY_LORA, RW_AAA_LORA))
    w = -jax.nn.softplus(-(w0 + jnp.tanh(wd) @ w_up)) - 0.5
    decay = jnp.exp(-jnp.exp(w))
    a = jax.nn.sigmoid(a0 + ad @ a_up)
    heads = lambda t: t.reshape(bsz, seq, RW_HEADS, RW_HD)
    kk = heads(k * k_k)
    kk = kk / jnp.maximum(jnp.sqrt(jnp.sum(kk * kk, axis=-1, keepdims=True)), 1e-12)
    k = k * (1.0 + (a - 1.0) * k_a)
    r, k, v, decay, a = heads(r), heads(k), heads(v), heads(decay), heads(a)
    b = kk * a

    def step(state, inp):
        r_t, w_t, k_t, v_t, kk_t, b_t = inp
        sa = jnp.einsum('bhvk,bhk->bhv', state, -kk_t)
        state = (state * w_t[:, :, None, :] + sa[..., None] * b_t[:, :, None, :]
                 + v_t[..., None] * k_t[:, :, None, :])
        return state, jnp.einsum('bhvk,bhk->bhv', state, r_t)

    xs = tuple(jnp.moveaxis(t, 1, 0) for t in (r, decay, k, v, kk, b))
    _, y = lax.scan(step, jnp.zeros((bsz, RW_HEADS, RW_HD, RW_HD), F32), xs)
    y = jnp.moveaxis(y, 0, 1)
    mean = jnp.mean(y, axis=-1, keepdims=True)
    var = jnp.mean(jnp.square(y - mean), axis=-1, keepdims=True)
    y = ((y - mean) * lax.rsqrt(var + RW_GN_EPS)).reshape(bsz, seq, D_GROUP) * lnx_w + lnx_b
    bonus = jnp.sum(r * k * r_k, axis=-1, keepdims=True) * v
    y = y + bonus.reshape(bsz, seq, D_GROUP)
    return y * jax.nn.silu(g.astype(F32))


def _fox_group(p, g, f_bias, q_norm_w, k_norm_w):
    bsz, seq = p.shape[0], p.shape[1]
    q, k, v, f = _split(p, (D_GROUP, D_GROUP, D_GROUP, FX_HEADS))
    q = jnp.transpose(_rmsnorm(q.reshape(bsz, seq, FX_HEADS, FX_HD), q_norm_w), (0, 2, 1, 3))
    k = jnp.transpose(_rmsnorm(k.reshape(bsz, seq, FX_HEADS, FX_HD), k_norm_w), (0, 2, 1, 3))
    v = jnp.transpose(v.reshape(bsz, seq, FX_HEADS, FX_HD).astype(F32), (0, 2, 1, 3))
    log_f = jax.nn.log_sigmoid(f.astype(F32) + f_bias)
    c = jnp.transpose(jnp.cumsum(log_f, axis=1), (0, 2, 1))
    scale = FX_HD ** -0.5
    key_pos = jnp.arange(seq)

    def block(i):
        start = i * FX_BLOCK
        qb = lax.dynamic_slice_in_dim(q, start, FX_BLOCK, axis=2)
        cb = lax.dynamic_slice_in_dim(c, start, FX_BLOCK, axis=2)
        s = jnp.einsum('bhqd,bhkd->bhqk', qb, k) * scale + cb[..., None] - c[:, :, None, :]
        q_pos = start + jnp.arange(FX_BLOCK)
        s = jnp.where(q_pos[:, None] >= key_pos[None, :], s, -jnp.inf)
        return jnp.einsum('bhqk,bhkd->bhqd', jax.nn.softmax(s, axis=-1), v)

    o = lax.map(block, jnp.arange(seq // FX_BLOCK))
    o = jnp.transpose(o, (1, 0, 3, 2, 4)).reshape(bsz, seq, D_GROUP)
    return o * jax.nn.silu(g.astype(F32))


def _mamba2_group(p, z, conv_w, conv_b, dt_bias, a_log, d_skip, norm_w):
    bsz, seq = p.shape[0], p.shape[1]
    xbc, dt = _split(p.astype(F32), (SSM_CONV_DIM, SSM_HEADS))
    xbc = lax.conv_general_dilated(xbc, conv_w.astype(F32)[:, None, :], window_strides=(1,),
                                   padding=[(SSM_CONV - 1, 0)],
                                   dimension_numbers=('NWC', 'WIO', 'NWC'),
                                   feature_group_count=SSM_CONV_DIM)
    xbc = jax.nn.silu(xbc + conv_b)
    xs, bm, cm = _split(xbc, (D_GROUP, SSM_GROUPS * SSM_STATE, SSM_GROUPS * SSM_STATE))
    rep = SSM_HEADS // SSM_GROUPS
    x_h = xs.reshape(bsz, seq, SSM_HEADS, SSM_HD)
    bm = jnp.repeat(bm.reshape(bsz, seq, SSM_GROUPS, SSM_STATE), rep, axis=2)
    cm = jnp.repeat(cm.reshape(bsz, seq, SSM_GROUPS, SSM_STATE), rep, axis=2)
    dt = jax.nn.softplus(dt + dt_bias)
    da = dt * (-jnp.exp(a_log.astype(F32)))
    xdt = x_h * dt[..., None]
    n_chunks = seq // SSM_CHUNK
    chunk = lambda t: t.reshape((bsz, n_chunks, SSM_CHUNK) + t.shape[2:])
    xc, bc, cc = chunk(xdt), chunk(bm), chunk(cm)
    acs = jnp.cumsum(jnp.moveaxis(chunk(da), 3, 1), axis=-1)
    tril = jnp.tril(jnp.ones((SSM_CHUNK, SSM_CHUNK), bool))
    lmat = jnp.exp(jnp.where(tril, acs[..., :, None] - acs[..., None, :], -jnp.inf))
    cb = jnp.einsum('bclhn,bcshn->bhcls', cc, bc)
    y_diag = jnp.einsum('bhcls,bcshp->bclhp', cb * lmat, xc)
    decay_states = jnp.exp(acs[..., -1:] - acs)
    states = jnp.einsum('bclhn,bhcl,bclhp->bchpn', bc, decay_states, xc)
    chunk_decay = jnp.exp(acs[..., -1])

    def step(h, inp):
        st, dec = inp
        return h * dec[:, :, None, None] + st, h

    _, prev = lax.scan(step, jnp.zeros((bsz, SSM_HEADS, SSM_HD, SSM_STATE), F32),
                       (jnp.moveaxis(states, 1, 0), jnp.moveaxis(chunk_decay, 2, 0)))
    prev = jnp.moveaxis(prev, 0, 1)
    y_off = jnp.einsum('bclhn,bchpn,bhcl->bclhp', cc, prev, jnp.exp(acs))
    y = (y_diag + y_off).reshape(bsz, seq, SSM_HEADS, SSM_HD) + d_skip[:, None] * x_h
    y = y.reshape(bsz, seq, D_GROUP) * jax.nn.silu(z.astype(F32))
    yg = y.reshape(bsz, seq, SSM_GROUPS, D_GROUP // SSM_GROUPS)
    yg = yg * lax.rsqrt(jnp.mean(yg * yg, axis=-1, keepdims=True) + NORM_EPS)
    return yg.reshape(bsz, seq, D_GROUP) * norm_w


def _memory_group(q, g, mem, mem_norm_w, w_kv, q_norm_w, k_norm_w):
    bsz, seq = q.shape[0], q.shape[1]
    n_mem = mem.shape[1]
    q = _rmsnorm(q.reshape(bsz, seq, MEM_HEADS, MEM_HD), q_norm_w)
    m = _rmsnorm(mem, mem_norm_w).astype(mem.dtype)
    mk, mv = jnp.split(m @ w_kv, 2, axis=-1)
    mk = _rmsnorm(mk.reshape(bsz, n_mem, MEM_HEADS, MEM_HD), k_norm_w)
    mv = mv.reshape(bsz, n_mem, MEM_HEADS, MEM_HD).astype(F32)
    s = jnp.einsum('bshd,bmhd->bhsm', q, mk) * (MEM_HD ** -0.5)
    o = jnp.einsum('bhsm,bmhd->bshd', jax.nn.softmax(s, axis=-1), mv)
    return o.reshape(bsz, seq, D_GROUP) * jax.nn.silu(g.astype(F32))


def setup_inputs(seed: int = 0) -> dict:
    key = jax.random.key(seed)
    ks = jax.random.split(key, 32)
    n = lambda i, shape: jax.random.normal(ks[i], shape, F32)
    u = lambda i, shape, lo, hi: jax.random.uniform(ks[i], shape, F32, lo, hi)
    dt0 = jnp.exp(u(20, (DEPTH, SSM_HEADS), float(np.log(1e-3)), float(np.log(1e-1))))
    return {
        "x": n(0, (BATCH, SEQ, D_MODEL)),
        "mem": n(1, (BATCH, N_MEM, D_MODEL)),
        "norm_w": 1.0 + 0.02 * n(2, (DEPTH, D_MODEL)),
        "w_in": n(3, (DEPTH, D_MODEL, D_IN)) * D_MODEL ** -0.5,
        "rw_mu": u(4, (DEPTH, RW_COLS), 0.0, 1.0),
        "rw_w0": u(5, (DEPTH, D_GROUP), -6.5, -1.5),
        "rw_w_up": n(6, (DEPTH, RW_DECAY_LORA, D_GROUP)) * 0.1 * RW_DECAY_LORA ** -0.5,
        "rw_a0": 0.1 * n(7, (DEPTH, D_GROUP)),
        "rw_a_up": n(8, (DEPTH, RW_AAA_LORA, D_GROUP)) * 0.1 * RW_AAA_LORA ** -0.5,
        "rw_k_k": 0.85 + 0.05 * n(9, (DEPTH, D_GROUP)),
        "rw_k_a": 1.0 + 0.05 * n(10, (DEPTH, D_GROUP)),
        "rw_r_k": 0.1 * n(11, (DEPTH, RW_HEADS, RW_HD)),
        "rw_lnx_w": 1.0 + 0.02 * n(12, (DEPTH, D_GROUP)),
        "rw_lnx_b": 0.02 * n(13, (DEPTH, D_GROUP)),
        "fx_f_bias": u(14, (DEPTH, FX_HEADS), 1.0, 4.0),
        "fx_q_norm_w": 1.0 + 0.02 * n(15, (DEPTH, FX_HD)),
        "fx_k_norm_w": 1.0 + 0.02 * n(16, (DEPTH, FX_HD)),
        "ssm_conv_w": 0.5 * n(17, (DEPTH, SSM_CONV, SSM_CONV_DIM)),
        "ssm_conv_b": 0.02 * n(18, (DEPTH, SSM_CONV_DIM)),
        "ssm_dt_bias": dt0 + jnp.log(-jnp.expm1(-dt0)),
        "ssm_a_log": jnp.log(u(19, (DEPTH, SSM_HEADS), 1.0, 16.0)),
        "ssm_d": 1.0 + 0.02 * n(21, (DEPTH, SSM_HEADS)),
        "ssm_norm_w": 1.0 + 0.02 * n(22, (DEPTH, D_GROUP)),
        "mem_norm_w": 1.0 + 0.02 * n(23, (DEPTH, D_MODEL)),
        "mem_w_kv": n(24, (DEPTH, D_MODEL, 2 * D_GROUP)) * D_MODEL ** -0.5,
        "mem_q_norm_w": 1.0 + 0.02 * n(25, (DEPTH, MEM_HD)),
        "mem_k_norm_w": 1.0 + 0.02 * n(26, (DEPTH, MEM_HD)),
        "w_out": n(27, (DEPTH, D_MIX, D_MODEL)) * D_MIX ** -0.5 / np.sqrt(2.0 * DEPTH),
    }


def reference(x, mem, norm_w, w_in, rw_mu, rw_w0, rw_w_up, rw_a0, rw_a_up, rw_k_k, rw_k_a,
              rw_r_k, rw_lnx_w, rw_lnx_b, fx_f_bias, fx_q_norm_w, fx_k_norm_w, ssm_conv_w,
              ssm_conv_b, ssm_dt_bias, ssm_a_log, ssm_d, ssm_norm_w, mem_norm_w, mem_w_kv,
              mem_q_norm_w, mem_k_norm_w, w_out):
    for l in range(DEPTH):
        h = _rmsnorm(x, norm_w[l]).astype(x.dtype)
        proj = h @ w_in[l]
        gate, p_rw, p_fx, p_ssm, q_mem = _split(proj, IN_SPLITS)
        g_rw, g_fx, z_ssm, g_mem = jnp.split(gate, 4, axis=-1)
        y_rw = _rwkv7_group(p_rw, g_rw, rw_mu[l], rw_w0[l], rw_w_up[l], rw_a0[l], rw_a_up[l],
                            rw_k_k[l], rw_k_a[l], rw_r_k[l], rw_lnx_w[l], rw_lnx_b[l])
        y_fx = _fox_group(p_fx, g_fx, fx_f_bias[l], fx_q_norm_w[l], fx_k_norm_w[l])
        y_ssm = _mamba2_group(p_ssm, z_ssm, ssm_conv_w[l], ssm_conv_b[l], ssm_dt_bias[l],
                              ssm_a_log[l], ssm_d[l], ssm_norm_w[l])
        y_mem = _memory_group(q_mem, g_mem, mem, mem_norm_w[l], mem_w_kv[l],
                              mem_q_norm_w[l], mem_k_norm_w[l])
        y = jnp.concatenate([y_rw, y_fx, y_ssm, y_mem], axis=-1).astype(x.dtype)
        x = x + y @ w_out[l]
    return x
```

```python
import contextlib
import os
import numpy as np
import ml_dtypes
import concourse.bass as bass
import concourse.mybir as mybir
from concourse.bass_utils import run_bass_kernel_spmd

F32 = mybir.dt.float32
BF16 = mybir.dt.bfloat16
AF = mybir.ActivationFunctionType
ALU = mybir.AluOpType

D_MODEL = 2048
DEPTH = 4
NCORE = 8
NPC = 40
EPS = 1e-6
RW_GN_EPS = 64e-5


class Buf:
    __slots__ = ("name", "w", "r")

    def __init__(self, name=""):
        self.name = name
        self.w = None
        self.r = []


def _compress(toks):
    need = {}
    for k, v in toks:
        if need.get(k, 0) < v:
            need[k] = v
    return list(need.items())


class Sched:
    NDMA = 12

    def __init__(self, nc, stack):
        self.nc = nc
        self.E = {"pe": nc.tensor, "dve": nc.vector, "act": nc.scalar,
                  "pool": nc.gpsimd, "sp": nc.sync}
        self.sem = {}
        self.cnt = {}
        for k in self.E:
            self.sem[k] = nc.alloc_semaphore(name="s_" + k)
            self.cnt[k] = 0
        self.dq = {}
        for q in ("sp",):
            lst = []
            for i in range(self.NDMA):
                key = "d_%s%d" % (q, i)
                self.sem[key] = nc.alloc_semaphore(name=key)
                self.cnt[key] = 0
                lst.append(key)
            self.dq[q] = [lst, 0]
        self.seen = {k: {} for k in self.E}
        self.n_ins = 0

    def _wait(self, e, toks):
        need = {}
        for t in toks:
            if t is None:
                continue
            k, v = t
            if need.get(k, 0) < v:
                need[k] = v
        for k, v in need.items():
            if k == e and e == "pe":
                continue
            if self.seen[e].get(k, 0) < v:
                self.E[e].wait_ge(self.sem[k], v)
                self.seen[e][k] = v

    def _deps(self, e, reads, writes):
        toks = []
        for b in reads:
            toks.append(b.w)
        for b in writes:
            toks.append(b.w)
            for t in b.r:
                if t[0] != e:
                    toks.append(t)
        return toks

    def _post(self, tok, reads, writes):
        for b in reads:
            b.r.append(tok)
            if len(b.r) > 16:
                b.r = _compress(b.r)
        for b in writes:
            b.w = tok
            b.r = []
        self.n_ins += 1

    def op(self, e, fn, reads=(), writes=()):
        self._wait(e, self._deps(e, reads, writes))
        ins = fn()
        self.cnt[e] += 1
        ins.then_inc(self.sem[e], 1)
        tok = (e, self.cnt[e])
        self._post(tok, reads, writes)
        return tok

    def dma(self, q, out, in_, reads=(), writes=()):
        lst, idx = self.dq[q]
        key = lst[idx % len(lst)]
        self.dq[q][1] = idx + 1
        toks = self._deps(q, reads, writes)
        if self.cnt[key] > 0:
            toks.append((key, self.cnt[key]))
        self._wait(q, toks)
        ins = self.E[q].dma_start(out=out, in_=in_)
        self.cnt[key] += 16
        ins.then_inc(self.sem[key], 16)
        tok = (key, self.cnt[key])
        self._post(tok, reads, writes)
        return tok

    def finish(self):
        for q in self.dq:
            for key in self.dq[q][0]:
                if self.cnt[key]:
                    self.E["sp"].wait_ge(self.sem[key], self.cnt[key])
        self.nc.all_engine_barrier()


class KB:
    def __init__(self, nc, st):
        self.nc = nc
        self.st = st
        self.S = Sched(nc, st)
        self._n = 0

    def sb(self, name, shape, dt):
        return self.st.enter_context(self.nc.sbuf_tensor(name, list(shape), dt))

    def ps(self, name, shape, dt):
        return self.st.enter_context(self.nc.psum_tensor(name, list(shape), dt))

    def eng(self, e):
        return self.S.E[e]

    def mm(self, out, lhsT, rhs, R, W, start=True, stop=True):
        nc = self.nc
        return self.S.op("pe", lambda: nc.tensor.matmul(out, lhsT=lhsT, rhs=rhs, start=start, stop=stop), R, W)

    def tr(self, out, in_, ident, R, W):
        nc = self.nc
        return self.S.op("pe", lambda: nc.tensor.transpose(out, in_, ident), R, W)

    def act(self, out, in_, func, R, W, bias=None, scale=None, accum_out=None):
        nc = self.nc
        kw = {}
        if bias is not None:
            kw["bias"] = bias
        if scale is not None:
            kw["scale"] = scale
        if accum_out is not None:
            kw["accum_out"] = accum_out
        return self.S.op("act", lambda: nc.scalar.activation(out=out, in_=in_, func=func, **kw), R, W)

    def tt(self, e, out, in0, in1, op, R, W):
        E = self.eng(e)
        return self.S.op(e, lambda: E.tensor_tensor(out=out, in0=in0, in1=in1, op=op), R, W)

    def ts(self, e, out, in0, s1, op0, R, W, s2=None, op1=None):
        E = self.eng(e)
        if op1 is None:
            return self.S.op(e, lambda: E.tensor_scalar(out=out, in0=in0, scalar1=s1, scalar2=None, op0=op0), R, W)
        return self.S.op(e, lambda: E.tensor_scalar(out=out, in0=in0, scalar1=s1, scalar2=s2, op0=op0, op1=op1), R, W)

    def stt(self, out, in0, scalar, in1, op0, op1, R, W):
        nc = self.nc
        return self.S.op("dve", lambda: nc.vector.scalar_tensor_tensor(out=out, in0=in0, scalar=scalar, in1=in1, op0=op0, op1=op1), R, W)

    def cp(self, e, out, in_, R, W):
        nc = self.nc
        if e == "act":
            return self.S.op("act", lambda: nc.scalar.copy(out=out, in_=in_), R, W)
        E = self.eng(e)
        return self.S.op(e, lambda: E.tensor_copy(out=out, in_=in_), R, W)

    def memset(self, e, ap, val, W):
        E = self.eng(e)
        return self.S.op(e, lambda: E.memset(ap, val), [], W)

    def rsqrt_from(self, out, in_, R, W, scale, eps):
        self.act(out, in_, AF.Ln, R, W, bias=eps, scale=scale)
        self.act(out, out, AF.Exp, W, W, scale=-0.5)

    def consts(self):
        nc = self.nc
        c = {}
        b = Buf("consts")
        c["b"] = b
        onesf = self.sb("c_onesf", [128, 128], F32)
        self.memset("pool", onesf[:], 1.0, [b])
        onesb = self.sb("c_onesb", [128, 128], BF16)
        self.memset("pool", onesb[:], 1.0, [b])
        zf = self.sb("c_zf", [128, 128], F32)
        self.memset("pool", zf[:], 0.0, [b])
        idf = self.sb("c_idf", [128, 128], F32)
        self.S.op("pool", lambda: nc.gpsimd.affine_select(out=idf[:], in_=zf[:], compare_op=ALU.not_equal, fill=1.0, base=0, pattern=[[-1, 128]], channel_multiplier=1), [b], [b])
        idb = self.sb("c_idb", [128, 128], BF16)
        self.cp("pool", idb[:], idf[:], [b], [b])
        trii = self.sb("c_trii", [128, 128], F32)
        self.S.op("pool", lambda: nc.gpsimd.affine_select(out=trii[:], in_=onesf[:], compare_op=ALU.is_ge, fill=0.0, base=0, pattern=[[1, 128]], channel_multiplier=-1), [b], [b])
        tris = self.sb("c_tris", [128, 128], F32)
        self.S.op("pool", lambda: nc.gpsimd.affine_select(out=tris[:], in_=onesf[:], compare_op=ALU.is_gt, fill=0.0, base=0, pattern=[[1, 128]], channel_multiplier=-1), [b], [b])
        triib = self.sb("c_triib", [128, 128], BF16)
        self.cp("pool", triib[:], trii[:], [b], [b])
        mneg = self.sb("c_mneg", [128, 128], F32)
        self.S.op("pool", lambda: nc.gpsimd.affine_select(out=mneg[:], in_=zf[:], compare_op=ALU.is_ge, fill=-30000.0, base=0, pattern=[[1, 128]], channel_multiplier=-1), [b], [b])
        bo = self.sb("c_bo", [128, 128], F32)
        self.memset("pool", bo[:], 0.0, [b])
        self.memset("pool", bo[0:64, 0:64], 1.0, [b])
        self.memset("pool", bo[64:128, 64:128], 1.0, [b])
        c.update(onesf=onesf, onesb=onesb, zf=zf, idf=idf, idb=idb, trii=trii, tris=tris,
                 triib=triib, mneg=mneg, bo=bo)
        return c


def build_p1(TOK, stop=99):
    nc = bass.Bass("TRN2", target_bir_lowering=False)
    dr = lambda name, shape, dt, kind="ExternalInput": nc.dram_tensor(name, list(shape), dt, kind=kind).ap()
    NT = TOK // 128
    TG = min(TOK, 512)
    NTG = TOK // TG
    x = dr("x", [TOK, D_MODEL], F32)
    nw = dr("nw", [1, D_MODEL], F32)
    mem = dr("memx", [256, D_MODEL], F32)
    mnw = dr("mnw", [1, D_MODEL], F32)
    wkv = dr("wkv", [16, 128, 2048], F32)
    wq = dr("wq", [16, 128, 1024], F32)
    wg = dr("wg", [16, 128, 1024], F32)
    pcm = dr("pcm", [128, 4], F32)
    hT_o = dr("hT", [16, 128, TOK], BF16, "ExternalOutput")
    ym_o = dr("ymT", [8, 128, TOK], BF16, "ExternalOutput")
    with nc.cleanup_on_exit(), contextlib.ExitStack() as st:
        K = KB(nc, st)
        S = K.S
        C = K.consts()
        cb = C["b"]
        nwb = K.sb("nwb", [128, D_MODEL], F32); b_nwb = Buf()
        mnwb = K.sb("mnwb", [128, D_MODEL], F32); b_mnwb = Buf()
        pcs = K.sb("pcs", [128, 4], F32); b_pcs = Buf()
        S.dma("sp", nwb[:], nw[0:1, :].partition_broadcast(128), writes=[b_nwb])
        S.dma("sp", mnwb[:], mnw[0:1, :].partition_broadcast(128), writes=[b_mnwb])
        S.dma("sp", pcs[:], pcm[:, :], writes=[b_pcs])
        if stop <= 0:
            S.finish()
            return nc
        xt = [K.sb("xt%d" % i, [128, D_MODEL], F32) for i in range(2)]
        b_xt = [Buf(), Buf()]
        junk = K.sb("junk", [128, D_MODEL], BF16); b_junk = Buf()
        hb = K.sb("hb", [128, D_MODEL], BF16); b_hb = Buf()
        col = K.sb("col", [128, 2], F32); b_col = Buf()
        hTt = K.sb("hTt", [128, 16, TOK], BF16); b_hTt = Buf()
        mT = K.sb("mT", [128, 16, 256], BF16); b_mT = Buf()
        ptrs = [K.ps("ptr%d" % i, [128, 512], BF16) for i in range(2)]
        b_ptr = [Buf() for _ in range(2)]
        pbig = [K.ps("pbig%d" % i, [128, 512], F32) for i in range(4)]
        b_pbig = [Buf() for _ in range(4)]
        pbi = [0]

        def big():
            i = pbi[0] % 4
            pbi[0] += 1
            return pbig[i], b_pbig[i]

        def norm_tile(src_ap, wbc, b_w, dstT, dst_b, col0, it):
            t = xt[it % 2]; bt = b_xt[it % 2]
            S.dma("sp", t[:], src_ap, writes=[bt])
            K.act(junk[:], t[:], AF.Square, [bt], [b_junk, b_col], accum_out=col[:, 0:1])
            K.rsqrt_from(col[:, 1:2], col[:, 0:1], [b_col], [b_col], 1.0 / D_MODEL, EPS)
            K.stt(hb[:], t[:], col[:, 1:2], wbc[:], ALU.mult, ALU.mult, [bt, b_col, b_w], [b_hb])
            for gi in range(4):
                half = gi % 2
                pb = b_ptr[half]
                for k4 in range(4):
                    kb = gi * 4 + k4
                    K.tr(ptrs[half][:, k4 * 128:(k4 + 1) * 128],
                         hb[:, kb * 128:(kb + 1) * 128], C["idb"][:], [b_hb, cb], [pb])
                e = "dve" if gi % 2 == 0 else "act"
                K.cp(e, dstT[:, gi * 4:gi * 4 + 4, col0:col0 + 128],
                     ptrs[half][:, :].rearrange("p (k t) -> p k t", t=128), [pb], [dst_b])

        it = 0
        for tt in range(NT):
            norm_tile(x[tt * 128:(tt + 1) * 128, :], nwb, b_nwb, hTt, b_hTt, tt * 128, it); it += 1
        S.dma("sp", hT_o.rearrange("k p t -> p k t"), hTt[:], reads=[b_hTt])
        if stop <= 1:
            S.finish()
            return nc
        for mt in range(2):
            norm_tile(mem[mt * 128:(mt + 1) * 128, :], mnwb, b_mnwb, mT, b_mT, mt * 128, it); it += 1

        wst = K.sb("wst", [128, 16, 512], F32); b_wst = Buf()
        wbf = K.sb("wbf", [128, 16, 512], BF16); b_wbf = Buf()
        mkf = K.sb("mkf", [128, 8, 256], F32); b_mkf = Buf()
        mv = K.sb("mv", [128, 2, 1024], BF16); b_mv = Buf()
        mkn = K.sb("mkn", [128, 8, 256], BF16); b_mkn = Buf()

        def load_w(src, c0, ncol):
            S.dma("sp", wst[:, 0:8, 0:ncol], src[0:8, :, c0:c0 + ncol].rearrange("k p c -> p k c"), writes=[b_wst])
            S.dma("sp", wst[:, 8:16, 0:ncol], src[8:16, :, c0:c0 + ncol].rearrange("k p c -> p k c"), writes=[b_wst])
            K.cp("dve", wbf[:, 0:8, 0:ncol], wst[:, 0:8, 0:ncol], [b_wst], [b_wbf])
            K.cp("act", wbf[:, 8:16, 0:ncol], wst[:, 8:16, 0:ncol], [b_wst], [b_wbf])

        for ch in range(4):
            load_w(wkv, ch * 512, 512)
            if ch < 2:
                for c4 in range(4):
                    p, bp = big()
                    for kb in range(16):
                        K.mm(p[:, 0:256], wbf[:, kb, c4 * 128:(c4 + 1) * 128], mT[:, kb, :], [b_wbf, b_mT], [bp], start=(kb == 0), stop=(kb == 15))
                    K.cp("dve", mkf[:, ch * 4 + c4, :], p[:, 0:256], [bp], [b_mkf])
            else:
                for mt in range(2):
                    p, bp = big()
                    for kb in range(16):
                        K.mm(p[:, :], mT[:, kb, mt * 128:(mt + 1) * 128], wbf[:, kb, :], [b_wbf, b_mT], [bp], start=(kb == 0), stop=(kb == 15))
                    K.cp("act", mv[:, mt, (ch - 2) * 512:(ch - 1) * 512], p[:, :], [bp], [b_mv])
        TGM = max(TG, 256)
        if stop <= 2:
            S.finish()
            return nc
        sq = K.sb("sq", [128, 2, TGM], F32); b_sq = Buf()
        rs = K.sb("rs", [128, TGM], F32); b_rs = Buf()
        for hh in range(4):
            K.act(sq[:, :, 0:256], mkf[:, 2 * hh:2 * hh + 2, :], AF.Square, [b_mkf], [b_sq])
            p, bp = big()
            for b2 in range(2):
                K.mm(p[:, 0:256], C["onesf"][:], sq[:, b2, 0:256], [cb, b_sq], [bp], start=(b2 == 0), stop=(b2 == 1))
            K.rsqrt_from(rs[:, 0:256], p[:, 0:256], [bp], [b_rs], 1.0 / 256, EPS)
            for b2 in range(2):
                K.stt(mkn[:, 2 * hh + b2, :], mkf[:, 2 * hh + b2, :], pcs[:, 2 + b2:3 + b2], rs[:, 0:256], ALU.mult, ALU.mult, [b_mkf, b_pcs, b_rs], [b_mkn])

        if stop <= 3:
            S.finish()
            return nc
        qf = K.sb("qf", [128, 2, TOK], F32); b_qf = Buf()
        sg = K.sb("sg", [128, 2, TOK], BF16); b_sg = Buf()
        qn = K.sb("qn", [128, 2, TOK], BF16); b_qn = Buf()
        ymT = K.sb("ymTs", [128, 8, TOK], BF16); b_ym = Buf()
        pT = [K.sb("pT%d" % i, [128, TG], BF16) for i in range(2)]
        b_pT = [Buf(), Buf()]
        rden = K.sb("rden", [128, TG], F32); b_rden = Buf()
        otmp = K.sb("otmp", [128, TG], F32); b_otmp = Buf()
        for hh in range(4):
            load_w(wq, hh * 256, 256)
            for b2 in range(2):
                for tg in range(NTG):
                    p, bp = big()
                    for kb in range(16):
                        K.mm(p[:, 0:TG], wbf[:, kb, b2 * 128:(b2 + 1) * 128], hTt[:, kb, tg * TG:(tg + 1) * TG], [b_wbf, b_hTt], [bp], start=(kb == 0), stop=(kb == 15))
                    K.cp("dve", qf[:, b2, tg * TG:(tg + 1) * TG], p[:, 0:TG], [bp], [b_qf])
            load_w(wg, hh * 256, 256)
            for b2 in range(2):
                for tg in range(NTG):
                    p, bp = big()
                    for kb in range(16):
                        K.mm(p[:, 0:TG], wbf[:, kb, b2 * 128:(b2 + 1) * 128], hTt[:, kb, tg * TG:(tg + 1) * TG], [b_wbf, b_hTt], [bp], start=(kb == 0), stop=(kb == 15))
                    K.act(sg[:, b2, tg * TG:(tg + 1) * TG], p[:, 0:TG], AF.Silu, [bp], [b_sg])
            for tg in range(NTG):
                tsl = slice(tg * TG, (tg + 1) * TG)
                K.act(sq[:, :, 0:TG], qf[:, :, tsl], AF.Square, [b_qf], [b_sq])
                p, bp = big()
                for b2 in range(2):
                    K.mm(p[:, 0:TG], C["onesf"][:], sq[:, b2, 0:TG], [cb, b_sq], [bp], start=(b2 == 0), stop=(b2 == 1))
                K.rsqrt_from(rs[:, 0:TG], p[:, 0:TG], [bp], [b_rs], 1.0 / 256, EPS)
                for b2 in range(2):
                    K.stt(qn[:, b2, tsl], qf[:, b2, tsl], pcs[:, b2:b2 + 1], rs[:, 0:TG], ALU.mult, ALU.mult, [b_qf, b_pcs, b_rs], [b_qn])
                for mt in range(2):
                    p, bp = big()
                    for b2 in range(2):
                        K.mm(p[:, 0:TG], mkn[:, 2 * hh + b2, mt * 128:(mt + 1) * 128], qn[:, b2, tsl], [b_mkn, b_qn], [bp], start=(b2 == 0), stop=(b2 == 1))
                    K.act(pT[mt][:, :], p[:, 0:TG], AF.Exp, [bp], [b_pT[mt]], scale=1.0 / 16)
                p, bp = big()
                for mt in range(2):
                    K.mm(p[:, 0:TG], C["onesb"][:], pT[mt][:, :], [cb, b_pT[mt]], [bp], start=(mt == 0), stop=(mt == 1))
                S.op("dve", lambda: nc.vector.reciprocal(out=rden[:, :], in_=p[:, 0:TG]), [bp], [b_rden])
                for b2 in range(2):
                    p, bp = big()
                    for mt in range(2):
                        K.mm(p[:, 0:TG], mv[:, mt, (2 * hh + b2) * 128:(2 * hh + b2 + 1) * 128], pT[mt][:, :], [b_mv, b_pT[mt]], [bp], start=(mt == 0), stop=(mt == 1))
                    K.tt("dve", otmp[:, :], p[:, 0:TG], rden[:, :], ALU.mult, [bp, b_rden], [b_otmp])
                    K.tt("pool", ymT[:, 2 * hh + b2, tsl], otmp[:, :], sg[:, b2, tsl], ALU.mult, [b_otmp, b_sg], [b_ym])
        S.dma("sp", ym_o.rearrange("k p t -> p k t"), ymT[:], reads=[b_ym])
        S.finish()
    return nc


def build_p3(TOK):
    nc = bass.Bass("TRN2", target_bir_lowering=False)
    dr = lambda name, shape, dt, kind="ExternalInput": nc.dram_tensor(name, list(shape), dt, kind=kind).ap()
    NT = TOK // 128
    x = dr("x", [TOK, D_MODEL], F32)
    yT = dr("yT", [32, 128, TOK], BF16)
    wo = dr("wo", [32, 128, D_MODEL], F32)
    snw = dr("snw", [128, 8], F32)
    xo = dr("xo", [TOK, D_MODEL], F32, "ExternalOutput")
    with nc.cleanup_on_exit(), contextlib.ExitStack() as st:
        K = KB(nc, st)
        S = K.S
        C = K.consts()
        cb = C["b"]
        ys = K.sb("ys", [128, 32, TOK], BF16); b_ys = Buf()
        sn = K.sb("sn", [128, 8], F32); b_sn = Buf()
        S.dma("sp", sn[:], snw[:, :], writes=[b_sn])
        for q4 in range(4):
            S.dma("sp", ys[:, q4 * 8:(q4 + 1) * 8, :], yT[q4 * 8:(q4 + 1) * 8, :, :].rearrange("k p t -> p k t"), writes=[b_ys])
        pbig = [K.ps("pbig%d" % i, [128, 512], F32) for i in range(6)]
        b_pbig = [Buf() for _ in range(6)]
        pbi = [0]

        def big():
            i = pbi[0] % 6
            pbi[0] += 1
            return pbig[i], b_pbig[i]

        rsd = K.sb("rsd", [128, NT * 4], F32); b_rsd = Buf()
        sq = K.sb("sq", [128, 2, 128], BF16); b_sq = Buf()
        for tt in range(NT):
            p, bp = big()
            for g in range(4):
                K.act(sq[:, :, :], ys[:, 16 + 2 * g:18 + 2 * g, tt * 128:(tt + 1) * 128], AF.Square, [b_ys], [b_sq])
                for b2 in range(2):
                    K.mm(p[:, g:g + 1], sq[:, b2, :], C["onesb"][:, 0:1], [b_sq, cb], [bp], start=(b2 == 0), stop=(b2 == 1))
            K.rsqrt_from(rsd[:, tt * 4:(tt + 1) * 4], p[:, 0:4], [bp], [b_rsd], 1.0 / 256, EPS)
        wst = K.sb("wst", [128, 16, 512], F32); b_wst = Buf()
        wob = K.sb("wob", [128, 32, 512], BF16); b_wob = Buf()
        xp = [K.sb("xp%d" % i, [128, 512], F32) for i in range(2)]
        b_xp = [Buf(), Buf()]
        it = 0
        for ch in range(4):
            csl = slice(ch * 512, (ch + 1) * 512)
            for half in range(2):
                S.dma("sp", wst[:, 0:8, :], wo[half * 16:half * 16 + 8, :, csl].rearrange("k p c -> p k c"), writes=[b_wst])
                S.dma("sp", wst[:, 8:16, :], wo[half * 16 + 8:half * 16 + 16, :, csl].rearrange("k p c -> p k c"), writes=[b_wst])
                if half == 0:
                    K.cp("dve", wob[:, 0:8, :], wst[:, 0:8, :], [b_wst], [b_wob])
                    K.cp("act", wob[:, 8:16, :], wst[:, 8:16, :], [b_wst], [b_wob])
                else:
                    for k8 in range(8):
                        K.ts("dve" if k8 % 2 == 0 else "pool", wob[:, 16 + k8, :], wst[:, k8, :], sn[:, k8:k8 + 1], ALU.mult, [b_wst, b_sn], [b_wob])
                    K.cp("act", wob[:, 24:32, :], wst[:, 8:16, :], [b_wst], [b_wob])
            for tt in range(NT):
                tsl = slice(tt * 128, (tt + 1) * 128)
                xt = xp[it % 2]; bx = b_xp[it % 2]; it += 1
                S.dma("sp", xt[:], x[tsl, csl], writes=[bx])
                p, bp = big()
                blks = list(range(0, 16)) + list(range(24, 32))
                for n, kb in enumerate(blks):
                    K.mm(p[:, :], ys[:, kb, tsl], wob[:, kb, :], [b_ys, b_wob], [bp], start=(n == 0), stop=(n == len(blks) - 1))
                K.tt("dve", xt[:], p[:, :], xt[:], ALU.add, [bp, bx], [bx])
                for g in range(4):
                    p, bp = big()
                    for b2 in range(2):
                        K.mm(p[:, :], ys[:, 16 + 2 * g + b2, tsl], wob[:, 16 + 2 * g + b2, :], [b_ys, b_wob], [bp], start=(b2 == 0), stop=(b2 == 1))
                    K.stt(xt[:], p[:, :], rsd[:, tt * 4 + g:tt * 4 + g + 1], xt[:], ALU.mult, ALU.add, [bp, b_rsd, bx], [bx])
                S.dma("sp", xo[tsl, csl], xt[:], reads=[bx])
        S.finish()
    return nc


NBLK_OF = {"rw": 5, "fx": 4, "ssm": 4}


def build_p2(SEQ, mixer):
    STOP = int(os.environ.get("P2STOP", "99"))
    nc = bass.Bass("TRN2", target_bir_lowering=False)
    dr = lambda name, shape, dt, kind="ExternalInput": nc.dram_tensor(name, list(shape), dt, kind=kind).ap()
    NSEG = SEQ // 512
    NB = SEQ // 128
    nblk = NBLK_OF[mixer]
    NCOL = nblk * 128 + (2 if mixer != "rw" else 0)
    hT = dr("hT", [16, 128, SEQ], BF16)
    wm = dr("wm", [16, 128, NCOL], F32)
    pcd = dr("pc", [128, NPC], F32)
    if mixer == "rw":
        lwd = dr("lw", [128, 128], F32)
    yo = dr("yo", [128, SEQ], BF16, "ExternalOutput")
    with nc.cleanup_on_exit(), contextlib.ExitStack() as st:
        K = KB(nc, st)
        S = K.S
        C = K.consts()
        cb = C["b"]
        idb, onesf, onesb, bo, trii, tris, triib, mneg = (C[k] for k in ("idb", "onesf", "onesb", "bo", "trii", "tris", "triib", "mneg"))
        pc = K.sb("pcs", [128, NPC], F32); b_pc = Buf()
        S.dma("sp", pc[:], pcd[:, :], writes=[b_pc])
        wbf = K.sb("wbf", [128, 16, NCOL], BF16); b_wbf = Buf()
        wst = [K.sb("wst%d" % i, [128, NCOL], F32) for i in range(2)]
        b_wst = [Buf(), Buf()]
        for kb in range(16):
            S.dma("sp", wst[kb % 2][:], wm[kb, :, :], writes=[b_wst[kb % 2]])
            K.cp("dve" if kb % 2 == 0 else "act", wbf[:, kb, :], wst[kb % 2][:], [b_wst[kb % 2]], [b_wbf])
        if STOP <= 0:
            S.finish()
            return nc
        pbig = [K.ps("pbig%d" % i, [128, 512], F32) for i in range(3)]
        b_pbig = [Buf() for _ in range(3)]
        pbi = [0]

        def big():
            i = pbi[0] % 3
            pbi[0] += 1
            return pbig[i], b_pbig[i]
        psm = [K.ps("psm%d" % i, [128, 512], F32) for i in range(3)]
        sm_regs = []
        for i in range(3):
            for j in range(2):
                sm_regs.append((psm[i], j * 256, Buf()))
        smi = [0]

        def small():
            t, off, b = sm_regs[smi[0] % 6]
            smi[0] += 1
            return t, off, b
        ptb = K.ps("ptb", [128, 512], BF16)
        ptb_regs = [(k * 128, Buf()) for k in range(4)]
        pti = [0]

        def tbf():
            off, b = ptb_regs[pti[0] % 4]
            pti[0] += 1
            return off, b
        pacc = K.ps("pacc", [128, 512], F32); b_pacc = Buf()
        hseg = [K.sb("hseg%d" % i, [128, 16, 512], BF16) for i in range(2)]
        b_hseg = [Buf(), Buf()]

        def load_seg(s):
            S.dma("sp", hseg[s % 2][:], hT[:, :, s * 512:(s + 1) * 512].rearrange("k p t -> p k t"), writes=[b_hseg[s % 2]])

        def inproj(s, blk):
            p, bp = big()
            for kb in range(16):
                K.mm(p[:, :], wbf[:, kb, blk * 128:(blk + 1) * 128], hseg[s % 2][:, kb, :], [b_wbf, b_hseg[s % 2]], [bp], start=(kb == 0), stop=(kb == 15))
            return p, bp

        yseg = [K.sb("yseg%d" % i, [128, 512], BF16) for i in range(2)]
        b_yseg = [Buf(), Buf()]
        sgt = K.sb("sgt", [128, 512], F32); b_sg = Buf()

        if mixer != "rw":
            misc = K.sb("misc", [128, NB, 2], F32); b_misc = Buf()
            pm, bpm = pacc, b_pacc
            for s in range(NSEG):
                load_seg(s)
                for jb in range(4):
                    j = s * 4 + jb
                    for kb in range(16):
                        K.mm(pm[:, (j % 128) * 2:(j % 128) * 2 + 2], hseg[s % 2][:, kb, jb * 128:(jb + 1) * 128], wbf[:, kb, nblk * 128:nblk * 128 + 2], [b_hseg[s % 2], b_wbf], [bpm], start=(kb == 0), stop=(kb == 15))
            K.cp("dve", misc[:, :, :], pm[:, 0:NB * 2].rearrange("p (j c) -> p j c", c=2), [bpm], [b_misc])
            if STOP <= 1:
                S.finish()
                return nc
            tmpa = K.sb("tmpa", [128, NB], F32); b_tmpa = Buf()
            col = K.sb("colx", [128, 4], F32); b_col = Buf()

        if mixer == "fx":
            nlf = [K.sb("nlf%d" % h, [128, NB], F32) for h in range(2)]; b_nlf = [Buf(), Buf()]
            pcs = [K.sb("pcs%d" % h, [128, NB], F32) for h in range(2)]; b_pcs = [Buf(), Buf()]
            Rs = [K.sb("Rs%d" % h, [128, NB], F32) for h in range(2)]; b_Rs = [Buf(), Buf()]
            Zs = K.sb("Zs", [128, 128], F32); b_Zs = Buf()
            K.ts("dve", col[:, 0:2], pc[:, 29:31], -1.0, ALU.mult, [b_pc], [b_col])
            for h in range(2):
                K.act(tmpa[:, :], misc[:, :, h], AF.Exp, [b_misc, b_col], [b_tmpa], bias=col[:, h:h + 1], scale=-1.0)
                K.act(nlf[h][:, :], tmpa[:, :], AF.Ln, [b_tmpa], [b_nlf[h]], bias=1.0)
                p1, bp1 = big()
                K.mm(p1[:, 0:NB], onesf[:, :], nlf[h][:, :], [cb, b_nlf[h]], [bp1])
                p2, bp2 = big()
                K.mm(p2[:, 0:NB], trii[:, :], nlf[h][:, :], [cb, b_nlf[h]], [bp2])
                K.cp("dve", tmpa[:, :], p1[:, 0:NB], [bp1], [b_tmpa])
                S.op("dve", lambda: nc.vector.tensor_tensor_scan(out=Rs[h][:, :], data0=onesf[:, 0:NB], data1=tmpa[:, :], initial=0.0, op0=ALU.mult, op1=ALU.add), [cb, b_tmpa], [b_Rs[h]])
                K.tt("dve", tmpa[:, :], Rs[h][:, :], tmpa[:, :], ALU.subtract, [b_Rs[h], b_tmpa], [b_tmpa])
                K.tt("dve", pcs[h][:, :], p2[:, 0:NB], tmpa[:, :], ALU.add, [bp2, b_tmpa], [b_pcs[h]])
            if STOP <= 2:
                S.finish()
                return nc
            kT = K.sb("kT", [128, SEQ], BF16); b_kT = Buf()
            vaug = K.sb("vaug", [128, NB, 2, 72], BF16); b_vaug = Buf()
            K.memset("dve", vaug[:, :, :, :], 1.0, [b_vaug])
            fq = K.sb("fq", [128, 512], F32); b_fq = Buf()
            sqt = K.sb("sqt", [128, 512], F32); b_sqt = Buf()
            rst = K.sb("rst", [128, 512], F32); b_rst = Buf()
            qnb = K.sb("qnb", [128, 512], BF16); b_qnb = Buf()
            fvb = K.sb("fvb", [128, 512], BF16); b_fvb = Buf()
            Bi = [K.sb("Bi%d" % i, [128, NB], F32) for i in range(2)]; b_Bi = [Buf(), Buf()]
            pTs = [K.sb("pTs%d" % i, [128, 128], BF16) for i in range(4)]; b_pTs = [Buf() for _ in range(4)]
            on = K.sb("on", [128, 128], BF16); b_on = Buf()
            rec = K.sb("rec", [128, 2], F32); b_rec = Buf()
            npt = 0
            nbi = 0
            for s in range(NSEG):
                load_seg(s)
                ys = yseg[s % 2]; bys = b_yseg[s % 2]
                if STOP <= 21:
                    S.finish()
                    return nc
                p, bp = inproj(s, 0)
                K.act(sgt[:, :], p[:, :], AF.Silu, [bp], [b_sg])
                if STOP <= 22:
                    S.finish()
                    return nc
                for which in (1, 2):
                    p, bp = inproj(s, which)
                    K.cp("dve", fq[:, :], p[:, :], [bp], [b_fq])
                    K.act(sqt[:, :], fq[:, :], AF.Square, [b_fq], [b_sqt])
                    if STOP <= 23:
                        S.finish()
                        return nc
                    p2, bp2 = big()
                    K.mm(p2[:, :], bo[:, :], sqt[:, :], [cb, b_sqt], [bp2])
                    if STOP <= 24:
                        S.finish()
                        return nc
                    K.rsqrt_from(rst[:, :], p2[:, :], [bp2], [b_rst], 1.0 / 64, EPS)
                    if STOP <= 25:
                        S.finish()
                        return nc
                    if which == 1:
                        K.stt(qnb[:, :], fq[:, :], pc[:, 11:12], rst[:, :], ALU.mult, ALU.mult, [b_fq, b_pc, b_rst], [b_qnb])
                    else:
                        K.stt(kT[:, s * 512:(s + 1) * 512], fq[:, :], pc[:, 12:13], rst[:, :], ALU.mult, ALU.mult, [b_fq, b_pc, b_rst], [b_kT])
                if STOP <= 3:
                    S.finish()
                    return nc
                p, bp = inproj(s, 3)
                K.cp("act", fvb[:, :], p[:, :], [bp], [b_fvb])
                for jb in range(4):
                    off, bt = tbf()
                    K.tr(ptb[:, off:off + 128], fvb[:, jb * 128:(jb + 1) * 128], idb[:, :], [b_fvb, cb], [bt])
                    K.cp("dve", vaug[:, s * 4 + jb, :, 0:64], ptb[:, off:off + 128].rearrange("p (h d) -> p h d", h=2), [bt], [b_vaug])
                if STOP <= 4:
                    S.finish()
                    return nc
                for qb in range(4):
                    i = s * 4 + qb
                    for h in range(2):
                        hs = slice(h * 64, (h + 1) * 64)
                        bi = Bi[nbi % 2]; bbi = b_Bi[nbi % 2]; nbi += 1
                        K.ts("pool", bi[:, 0:i + 1], pcs[h][:, 0:i + 1], Rs[h][:, i:i + 1], ALU.subtract, [b_pcs[h], b_Rs[h]], [bbi])
                        for j in range(i + 1):
                            t, off, bs = small()
                            K.mm(t[:, off:off + 128], kT[hs, j * 128:(j + 1) * 128], qnb[hs, qb * 128:(qb + 1) * 128], [b_kT, b_qnb], [bs])
                            pt = pTs[npt % 4]; bpt = b_pTs[npt % 4]; npt += 1
                            K.act(pt[:, :], t[:, off:off + 128], AF.Exp, [bs, bbi], [bpt], bias=bi[:, j:j + 1], scale=0.125)
                            if j == i:
                                K.tt("pool", pt[:, :], pt[:, :], triib[:, :], ALU.mult, [bpt, cb], [bpt])
                            K.mm(pacc[:, 0:65], pt[:, :], vaug[:, j, h, 0:65], [bpt, b_vaug], [b_pacc], start=(j == 0), stop=(j == i))
                        S.op("dve", lambda: nc.vector.reciprocal(out=rec[:, h:h + 1], in_=pacc[:, 64:65]), [b_pacc], [b_rec])
                        K.ts("dve", on[:, hs], pacc[:, 0:64], rec[:, h:h + 1], ALU.mult, [b_pacc, b_rec], [b_on])
                    off, bt = tbf()
                    K.tr(ptb[:, off:off + 128], on[:, :], idb[:, :], [b_on, cb], [bt])
                    K.tt("dve", ys[:, qb * 128:(qb + 1) * 128], ptb[:, off:off + 128], sgt[:, qb * 128:(qb + 1) * 128], ALU.mult, [bt, b_sg], [bys])
                S.dma("sp", yo[:, s * 512:(s + 1) * 512], ys[:, :], reads=[bys])

        if mixer == "ssm":
            dt = [K.sb("dt%d" % h, [128, NB], F32) for h in range(2)]; b_dt = [Buf(), Buf()]
            da = [K.sb("da%d" % h, [128, NB], F32) for h in range(2)]; b_da = [Buf(), Buf()]
            nacs = [K.sb("nacs%d" % h, [128, NB], F32) for h in range(2)]; b_nacs = [Buf(), Buf()]
            alast = [K.sb("alast%d" % h, [128, NB], F32) for h in range(2)]; b_alast = [Buf(), Buf()]
            cdec = [K.sb("cdec%d" % h, [128, NB], F32) for h in range(2)]; b_cdec = [Buf(), Buf()]
            wgt = [K.sb("wgt%d" % h, [128, NB], F32) for h in range(2)]; b_wgt = [Buf(), Buf()]
            K.act(col[:, 0:2], pc[:, 33:35], AF.Exp, [b_pc], [b_col])
            for h in range(2):
                K.act(tmpa[:, :], misc[:, :, h], AF.Exp, [b_misc, b_pc], [b_tmpa], bias=pc[:, 31 + h:32 + h])
                K.act(dt[h][:, :], tmpa[:, :], AF.Ln, [b_tmpa], [b_dt[h]], bias=1.0)
                K.ts("dve", da[h][:, :], dt[h][:, :], col[:, h:h + 1], ALU.mult, [b_dt[h], b_col], [b_da[h]], s2=-1.0, op1=ALU.mult)
                p1, bp1 = big()
                K.mm(p1[:, 0:NB], trii[:, :], da[h][:, :], [cb, b_da[h]], [bp1])
                K.ts("dve", nacs[h][:, :], p1[:, 0:NB], -1.0, ALU.mult, [bp1], [b_nacs[h]])
                p2, bp2 = big()
                K.mm(p2[:, 0:NB], onesf[:, :], da[h][:, :], [cb, b_da[h]], [bp2])
                K.cp("dve", alast[h][:, :], p2[:, 0:NB], [bp2], [b_alast[h]])
                K.act(cdec[h][:, :], alast[h][:, :], AF.Exp, [b_alast[h]], [b_cdec[h]])
                K.tt("dve", tmpa[:, :], alast[h][:, :], nacs[h][:, :], ALU.add, [b_alast[h], b_nacs[h]], [b_tmpa])
                K.act(tmpa[:, :], tmpa[:, :], AF.Exp, [b_tmpa], [b_tmpa])
                K.tt("dve", wgt[h][:, :], tmpa[:, :], dt[h][:, :], ALU.mult, [b_tmpa, b_dt[h]], [b_wgt[h]])
            raw = [K.sb("raw%d" % i, [128, 515], F32) for i in range(3)]; b_raw = [Buf() for _ in range(3)]
            for i in range(3):
                K.memset("pool", raw[i][:, 0:3], 0.0, [b_raw[i]])
            acc = K.sb("acc", [128, 512], F32); b_acc = Buf()
            xc = K.sb("xc", [128, 512], F32); b_xc = Buf()
            xcb = K.sb("xcb", [128, 512], BF16); b_xcb = Buf()
            BcT = K.sb("BcT", [128, 512], BF16); b_BcT = Buf()
            CcT = K.sb("CcT", [128, 512], BF16); b_CcT = Buf()
            xdt = K.sb("xdt", [128, 128], BF16); b_xdt = Buf()
            xdd = K.sb("xdd", [128, 128], BF16); b_xdd = Buf()
            Btok = K.sb("Btok", [128, 128], BF16); b_Btok = Buf()
            rhsj = K.sb("rhsj", [128, 128], F32); b_rhsj = Buf()
            Erow = K.sb("Erow", [128, 128], F32); b_Erow = Buf()
            arow = K.sb("arow", [128, 128], F32); b_arow = Buf()
            tmpD = K.sb("tmpD", [128, 128], F32); b_tmpD = Buf()
            Lm = K.sb("Lm", [128, 128], F32); b_Lm = Buf()
            MT = K.sb("MT", [128, 128], BF16); b_MT = Buf()
            CdT = K.sb("CdT", [128, 128], BF16); b_CdT = Buf()
            hTf = K.sb("hTf", [128, 128], F32); b_hTf = Buf()
            hTb = K.sb("hTb", [128, 128], BF16); b_hTb = Buf()
            K.memset("pool", hTf[:, :], 0.0, [b_hTf])
            K.memset("pool", hTb[:, :], 0.0, [b_hTb])
            ytmp = K.sb("ytmp", [128, 128], F32); b_ytmp = Buf()
            for s in range(NSEG):
                load_seg(s)
                ys = yseg[s % 2]; bys = b_yseg[s % 2]
                p, bp = inproj(s, 0)
                K.act(sgt[:, :], p[:, :], AF.Silu, [bp], [b_sg])
                for i in range(3):
                    p, bp = inproj(s, 1 + i)
                    K.cp("dve", raw[i][:, 3:515], p[:, :], [bp], [b_raw[i]])
                    c0 = 13 + 5 * i
                    K.ts("dve", acc[:, :], raw[i][:, 0:512], pc[:, c0:c0 + 1], ALU.mult, [b_raw[i], b_pc], [b_acc])
                    for tap in range(1, 4):
                        K.stt(acc[:, :], raw[i][:, tap:tap + 512], pc[:, c0 + tap:c0 + tap + 1], acc[:, :], ALU.mult, ALU.add, [b_raw[i], b_pc, b_acc], [b_acc])
                    dst, bd = ((xc, b_xc), (BcT, b_BcT), (CcT, b_CcT))[i]
                    K.act(dst[:, :], acc[:, :], AF.Silu, [b_acc, b_pc], [bd], bias=pc[:, c0 + 4:c0 + 5])
                    K.cp("pool", raw[i][:, 0:3], raw[i][:, 512:515], [b_raw[i]], [b_raw[i]])
                K.cp("pool", xcb[:, :], xc[:, :], [b_xc], [b_xcb])
                for jb in range(4):
                    j = s * 4 + jb
                    cs = slice(jb * 128, (jb + 1) * 128)
                    off, bt = tbf()
                    K.tr(ptb[:, off:off + 128], xcb[:, cs], idb[:, :], [b_xcb, cb], [bt])
                    for h in range(2):
                        hs = slice(h * 64, (h + 1) * 64)
                        K.ts("dve", xdt[:, hs], ptb[:, off + h * 64:off + h * 64 + 64], dt[h][:, j:j + 1], ALU.mult, [bt, b_dt[h]], [b_xdt])
                        K.ts("dve", xdd[:, hs], ptb[:, off + h * 64:off + h * 64 + 64], wgt[h][:, j:j + 1], ALU.mult, [bt, b_wgt[h]], [b_xdd])
                    off2, bt2 = tbf()
                    K.tr(ptb[:, off2:off2 + 128], BcT[:, cs], idb[:, :], [b_BcT, cb], [bt2])
                    K.cp("dve", Btok[:, :], ptb[:, off2:off2 + 128], [bt2], [b_Btok])
                    tcb, offcb, bcb = small()
                    K.mm(tcb[:, offcb:offcb + 128], BcT[:, cs], CcT[:, cs], [b_BcT, b_CcT], [bcb])
                    for h in range(2):
                        hs = slice(h * 64, (h + 1) * 64)
                        K.ts("pool", rhsj[:, :], trii[:, :], da[h][:, j:j + 1], ALU.mult, [cb, b_da[h]], [b_rhsj])
                        t, off, bs = small()
                        K.mm(t[:, off:off + 128], onesf[:, :], rhsj[:, :], [cb, b_rhsj], [bs])
                        K.cp("dve", arow[:, :], t[:, off:off + 128], [bs], [b_arow])
                        K.act(Erow[:, :], arow[:, :], AF.Exp, [b_arow], [b_Erow])
                        K.tt("pool", tmpD[:, :], arow[:, :], mneg[:, :], ALU.add, [b_arow, cb], [b_tmpD])
                        K.act(Lm[:, :], tmpD[:, :], AF.Exp, [b_tmpD, b_nacs[h]], [b_Lm], bias=nacs[h][:, j:j + 1])
                        K.tt("dve", MT[:, :], tcb[:, offcb:offcb + 128], Lm[:, :], ALU.mult, [bcb, b_Lm], [b_MT])
                        K.tt("pool", CdT[:, :], CcT[:, cs], Erow[:, :], ALU.mult, [b_CcT, b_Erow], [b_CdT])
                        K.mm(pacc[hs, 0:128], xdt[:, hs], MT[:, :], [b_xdt, b_MT], [b_pacc], start=True, stop=False)
                        K.mm(pacc[hs, 0:128], hTb[:, hs], CdT[:, :], [b_hTb, b_CdT], [b_pacc], start=False, stop=True)
                    t, off, bs = small()
                    K.mm(t[:, off:off + 128], Btok[:, :], xdd[:, :], [b_Btok, b_xdd], [bs])
                    for h in range(2):
                        hs = slice(h * 64, (h + 1) * 64)
                        K.stt(hTf[:, hs], hTf[:, hs], cdec[h][:, j:j + 1], t[:, off + h * 64:off + h * 64 + 64], ALU.mult, ALU.add, [b_hTf, b_cdec[h], bs], [b_hTf])
                    K.cp("act", hTb[:, :], hTf[:, :], [b_hTf], [b_hTb])
                    K.stt(ytmp[:, :], xc[:, cs], pc[:, 28:29], pacc[:, 0:128], ALU.mult, ALU.add, [b_xc, b_pc, b_pacc], [b_ytmp])
                    K.tt("dve", ys[:, cs], ytmp[:, :], sgt[:, cs], ALU.mult, [b_ytmp, b_sg], [bys])
                S.dma("sp", yo[:, s * 512:(s + 1) * 512], ys[:, :], reads=[bys])

        if mixer == "rw":
            f32t = lambda name, w=512: K.sb(name, [128, w], F32)
            colr = K.sb("colr", [128, 8], F32); b_colr = Buf()
            K.ts("dve", colr[:, 0:4], pc[:, 0:4], -1.0, ALU.mult, [b_pc], [b_colr], s2=1.0, op1=ALU.add)
            K.ts("dve", colr[:, 4:5], pc[:, 7:8], -1.0, ALU.mult, [b_pc], [b_colr], s2=1.0, op1=ALU.add)
            lws = K.sb("lws", [128, 128], F32); b_lws = Buf()
            lwb = K.sb("lwb", [128, 128], BF16); b_lwb = Buf()
            S.dma("sp", lws[:], lwd[:, :], writes=[b_lws])
            K.cp("dve", lwb[:, :], lws[:, :], [b_lws], [b_lwb])
            mb = Buf("rwmasks")
            slf = K.sb("slf", [128, 128], F32)
            K.ts("dve", slf[:, :], trii[:, :], -1.0, ALU.mult, [cb], [mb], s2=1.0, op1=ALU.add)
            maskall = K.sb("maskall", [128, 256], F32)
            sl64 = K.sb("sl64", [128, 64], F32)
            Istk = K.sb("Istk", [128, 64], F32)
            for h in range(2):
                hs = slice(h * 64, (h + 1) * 64)
                for q4, src in enumerate((tris, trii, tris, trii)):
                    K.cp("dve", maskall[hs, q4 * 64:(q4 + 1) * 64], src[hs, hs], [cb], [mb])
                K.cp("dve", sl64[hs, :], slf[hs, hs], [mb], [mb])
                K.cp("dve", Istk[hs, :], C["idf"][hs, hs], [cb], [mb])
            rm = f32t("rm")
            K.memset("dve", rm[:, :], 1.0, [mb])
            for c8 in range(8):
                K.memset("dve", rm[:, c8 * 64:c8 * 64 + 1], 0.0, [mb])
            rawr = [K.sb("rawr%d" % i, [128, 513], F32) for i in range(4)]; b_rawr = [Buf() for _ in range(4)]
            for i in range(4):
                K.memset("pool", rawr[i][:, 0:1], 0.0, [b_rawr[i]])
            xs_ = [f32t("xs%d" % i) for i in range(4)]; b_xs = [Buf() for _ in range(4)]
            xr, xk, xv, xwd = xs_
            b_xr, b_xk, b_xv, b_xwd = b_xs
            tmp5 = f32t("tmp5"); b_tmp5 = Buf()
            lin = K.sb("lin", [128, 512], BF16); b_lin = Buf()
            sw = f32t("sw"); b_sw = Buf()
            lwt = f32t("lwt"); b_lwt = Buf()
            at = f32t("at"); b_at = Buf()
            kkt = f32t("kkt"); b_kkt = Buf()
            sqt = f32t("sqt"); b_sqt = Buf()
            rnt = f32t("rnt"); b_rnt = Buf()
            kkn = f32t("kkn"); b_kkn = Buf()
            kmod = f32t("kmod"); b_kmod = Buf()
            bt_ = f32t("bt_"); b_bt = Buf()
            ci = f32t("ci"); b_ci = Buf()
            ce = f32t("ce"); b_ce = Buf()
            E1 = f32t("E1"); b_E1 = Buf()
            E2 = f32t("E2"); b_E2 = Buf()
            E3 = f32t("E3"); b_E3 = Buf()
            rkr = f32t("rkr"); b_rkr = Buf()
            bont = f32t("bont"); b_bont = Buf()
            yraw = f32t("yraw"); b_yraw = Buf()
            ym = f32t("ym"); b_ym = Buf()
            AR = K.sb("AR", [128, 8, 128], BF16); b_AR = Buf()
            Bt = K.sb("Bt", [128, 512], BF16); b_Bt = Buf()
            Kt = K.sb("Kt", [128, 512], BF16); b_Kt = Buf()
            Vb = K.sb("Vb", [128, 512], BF16); b_Vb = Buf()
            tokT = [K.sb("tokT%d" % i, [128, 256], BF16) for i in range(2)]; b_tokT = [Buf(), Buf()]
            Am = [K.sb("Am%d" % i, [128, 256], BF16) for i in range(2)]; b_Am = [Buf(), Buf()]
            AnT0 = [K.sb("AnT0%d" % i, [128, 64], BF16) for i in range(2)]; b_AnT0 = [Buf(), Buf()]
            A2 = [K.sb("A2_%d" % i, [128, 128], BF16) for i in range(4)]; b_A2 = [Buf() for _ in range(4)]
            Pt = [K.sb("Pt%d" % i, [128, 64], BF16) for i in range(4)]; b_Pt = [Buf() for _ in range(4)]
            QA = [K.sb("QA%d" % i, [128, 128], BF16) for i in range(2)]; b_QA = [Buf(), Buf()]
            UT = [K.sb("UT%d" % i, [128, 64], BF16) for i in range(2)]; b_UT = [Buf(), Buf()]
            Sf = K.sb("Sf", [128, 64], F32); b_Sf = Buf()
            Sb = K.sb("Sb", [128, 64], BF16); b_Sb = Buf()
            tmpS = K.sb("tmpS", [128, 64], F32); b_tmpS = Buf()
            K.memset("pool", Sf[:, :], 0.0, [b_Sf])
            K.memset("pool", Sb[:, :], 0.0, [b_Sb])
            HS = [slice(0, 64), slice(64, 128)]
            na2 = 0
            npt_ = 0
            nch = 0
            for s in range(NSEG):
                load_seg(s)
                ys = yseg[s % 2]; bys = b_yseg[s % 2]
                p, bp = inproj(s, 0)
                K.act(sgt[:, :], p[:, :], AF.Silu, [bp], [b_sg])
                for i in range(4):
                    p, bp = inproj(s, 1 + i)
                    K.cp("dve", rawr[i][:, 1:513], p[:, :], [bp], [b_rawr[i]])
                    K.ts("dve", tmp5[:, :], rawr[i][:, 0:512], pc[:, i:i + 1], ALU.mult, [b_rawr[i], b_pc], [b_tmp5])
                    K.stt(xs_[i][:, :], rawr[i][:, 1:513], colr[:, i:i + 1], tmp5[:, :], ALU.mult, ALU.add, [b_rawr[i], b_colr, b_tmp5], [b_xs[i]])
                    K.cp("pool", rawr[i][:, 0:1], rawr[i][:, 512:513], [b_rawr[i]], [b_rawr[i]])
                K.act(lin[0:64, :], xwd[0:64, :], AF.Tanh, [b_xwd], [b_lin])
                K.cp("pool", lin[64:128, :], xwd[64:128, :], [b_xwd], [b_lin])
                p, bp = big()
                K.mm(p[:, :], lwb[0:64, :], lin[0:64, :], [b_lwb, b_lin], [bp])
                K.act(sw[:, :], p[:, :], AF.Sigmoid, [bp, b_pc], [b_sw], bias=pc[:, 4:5])
                K.ts("dve", lwt[:, :], sw[:, :], -0.6065306597126334, ALU.mult, [b_sw], [b_lwt])
                p, bp = big()
                K.mm(p[:, :], lwb[64:128, :], lin[64:128, :], [b_lwb, b_lin], [bp])
                K.act(at[:, :], p[:, :], AF.Sigmoid, [bp, b_pc], [b_at], bias=pc[:, 5:6])
                K.ts("dve", kkt[:, :], xk[:, :], pc[:, 6:7], ALU.mult, [b_xk, b_pc], [b_kkt])
                K.act(sqt[:, :], kkt[:, :], AF.Square, [b_kkt], [b_sqt])
                p, bp = big()
                K.mm(p[:, :], bo[:, :], sqt[:, :], [cb, b_sqt], [bp])
                K.rsqrt_from(rnt[:, :], p[:, :], [bp], [b_rnt], 1.0, 1e-24)
                K.tt("dve", kkn[:, :], kkt[:, :], rnt[:, :], ALU.mult, [b_kkt, b_rnt], [b_kkn])
                K.ts("dve", tmp5[:, :], at[:, :], pc[:, 7:8], ALU.mult, [b_at, b_pc, b_colr], [b_tmp5], s2=colr[:, 4:5], op1=ALU.add)
                K.tt("dve", kmod[:, :], tmp5[:, :], xk[:, :], ALU.mult, [b_tmp5, b_xk], [b_kmod])
                K.tt("pool", bt_[:, :], kkn[:, :], at[:, :], ALU.mult, [b_kkn, b_at], [b_bt])
                S.op("dve", lambda: nc.vector.tensor_tensor_scan(out=ci[:, :], data0=rm[:, :], data1=lwt[:, :], initial=0.0, op0=ALU.mult, op1=ALU.add), [mb, b_lwt], [b_ci])
                K.tt("pool", ce[:, :], ci[:, :], lwt[:, :], ALU.subtract, [b_ci, b_lwt], [b_ce])
                K.act(E1[:, :], ci[:, :], AF.Exp, [b_ci], [b_E1])
                K.act(E2[:, :], ci[:, :], AF.Exp, [b_ci], [b_E2], scale=-1.0)
                K.act(E3[:, :], ce[:, :], AF.Exp, [b_ce], [b_E3])
                v3 = lambda ap: ap.rearrange("p (c l) -> p c l", l=64)
                K.stt(AR[:, :, 0:64], v3(kkn[:, :]), -1.0, v3(E3[:, :]), ALU.mult, ALU.mult, [b_kkn, b_E3], [b_AR])
                K.tt("dve", AR[:, :, 64:128], v3(xr[:, :]), v3(E1[:, :]), ALU.mult, [b_xr, b_E1], [b_AR])
                K.tt("dve", Bt[:, :], bt_[:, :], E2[:, :], ALU.mult, [b_bt, b_E2], [b_Bt])
                K.tt("pool", Kt[:, :], kmod[:, :], E2[:, :], ALU.mult, [b_kmod, b_E2], [b_Kt])
                K.cp("pool", Vb[:, :], xv[:, :], [b_xv], [b_Vb])
                K.stt(rkr[:, :], xr[:, :], pc[:, 8:9], kmod[:, :], ALU.mult, ALU.mult, [b_xr, b_pc, b_kmod], [b_rkr])
                p, bp = big()
                K.mm(p[:, :], bo[:, :], rkr[:, :], [cb, b_rkr], [bp])
                K.tt("dve", bont[:, :], p[:, :], xv[:, :], ALU.mult, [bp, b_xv], [b_bont])
                for c in range(8):
                    cc = slice(c * 64, (c + 1) * 64)
                    tk = tokT[nch % 2]; btk = b_tokT[nch % 2]
                    am = Am[nch % 2]; bam = b_Am[nch % 2]
                    an0 = AnT0[nch % 2]; ban0 = b_AnT0[nch % 2]
                    qa = QA[nch % 2]; bqa = b_QA[nch % 2]
                    ut = UT[nch % 2]; but = b_UT[nch % 2]
                    nch += 1
                    t, off, bs = small()
                    for h in range(2):
                        hs = HS[h]
                        K.mm(t[hs, off:off + 64], Vb[hs, cc], idb[hs, hs], [b_Vb, cb], [bs])
                        K.mm(t[hs, off + 64:off + 128], AR[hs, c, 0:64], idb[hs, hs], [b_AR, cb], [bs])
                        K.mm(t[hs, off + 128:off + 192], Bt[hs, cc], idb[hs, hs], [b_Bt, cb], [bs])
                        K.mm(t[hs, off + 192:off + 256], Kt[hs, cc], idb[hs, hs], [b_Kt, cb], [bs])
                    K.cp("dve", tk[:, :], t[:, off:off + 256], [bs], [btk])
                    t, off, bs = small()
                    for h in range(2):
                        hs = HS[h]
                        K.mm(t[hs, off:off + 128], Bt[hs, cc], AR[hs, c, :], [b_Bt, b_AR], [bs])
                        K.mm(t[hs, off + 128:off + 256], Kt[hs, cc], AR[hs, c, :], [b_Kt, b_AR], [bs])
                    K.tt("dve", am[:, :], t[:, off:off + 256], maskall[:, :], ALU.mult, [bs, mb], [bam])
                    t, off, bs = small()
                    for h in range(2):
                        hs = HS[h]
                        K.mm(t[hs, off:off + 64], AR[hs, c, 0:64], Bt[hs, cc], [b_AR, b_Bt], [bs])
                    K.tt("dve", an0[:, :], t[:, off:off + 64], sl64[:, :], ALU.mult, [bs, mb], [ban0])
                    P = Pt[npt_ % 4]; bP = b_Pt[npt_ % 4]; npt_ += 1
                    K.tt("dve", P[:, :], am[:, 0:64], Istk[:, :], ALU.add, [bam, mb], [bP])
                    curA, curAT, bcur = am[:, 0:64], an0[:, :], [bam, ban0]
                    for lvl in range(5):
                        a2 = A2[na2 % 4]; ba2 = b_A2[na2 % 4]; na2 += 1
                        t, off, bs = small()
                        for h in range(2):
                            hs = HS[h]
                            if lvl < 4:
                                K.mm(t[hs, off:off + 64], curAT[hs], curA[hs], bcur, [bs])
                            K.mm(t[hs, off + 64:off + 128], curA[hs], curAT[hs], bcur, [bs])
                        if lvl < 4:
                            K.cp("dve", a2[:, :], t[:, off:off + 128], [bs], [ba2])
                        else:
                            K.cp("dve", a2[:, 64:128], t[:, off + 64:off + 128], [bs], [ba2])
                        curA, curAT, bcur = a2[:, 0:64], a2[:, 64:128], [ba2]
                        t, off, bs = small()
                        for h in range(2):
                            hs = HS[h]
                            K.mm(t[hs, off:off + 64], curAT[hs], P[hs, :], [ba2, bP], [bs])
                        Pn = Pt[npt_ % 4]; bPn = b_Pt[npt_ % 4]; npt_ += 1
                        K.tt("dve", Pn[:, :], t[:, off:off + 64], P[:, :], ALU.add, [bs, bP], [bPn])
                        P, bP = Pn, bPn
                    t, off, bs = small()
                    for h in range(2):
                        hs = HS[h]
                        K.mm(t[hs, off:off + 64], am[hs, 128:192], tk[hs, 0:64], [bam, btk], [bs])
                        K.mm(t[hs, off + 64:off + 128], tk[hs, 64:128], P[hs, :], [btk, bP], [bs])
                    K.cp("dve", qa[:, :], t[:, off:off + 128], [bs], [bqa])
                    t, off, bs = small()
                    for h in range(2):
                        hs = HS[h]
                        K.mm(t[hs, off:off + 64], P[hs, :], qa[hs, 0:64], [bP, bqa], [bs], start=True, stop=False)
                        K.mm(t[hs, off:off + 64], qa[hs, 64:128], Sb[hs, :], [bqa, b_Sb], [bs], start=False, stop=True)
                    K.cp("dve", ut[:, :], t[:, off:off + 64], [bs], [but])
                    t, off, bs = small()
                    for h in range(2):
                        hs = HS[h]
                        K.mm(t[hs, off:off + 64], Sb[hs, :], AR[hs, c, 64:128], [b_Sb, b_AR], [bs], start=True, stop=False)
                        K.mm(t[hs, off:off + 64], ut[hs, :], am[hs, 64:128], [but, bam], [bs], start=False, stop=False)
                        K.mm(t[hs, off:off + 64], tk[hs, 0:64], am[hs, 192:256], [btk, bam], [bs], start=False, stop=True)
                    K.cp("dve", yraw[:, cc], t[:, off:off + 64], [bs], [b_yraw])
                    t, off, bs = small()
                    for h in range(2):
                        hs = HS[h]
                        K.mm(t[hs, off:off + 64], tk[hs, 192:256], tk[hs, 0:64], [btk], [bs], start=True, stop=False)
                        K.mm(t[hs, off:off + 64], tk[hs, 128:192], ut[hs, :], [btk, but], [bs], start=False, stop=True)
                    K.tt("dve", tmpS[:, :], t[:, off:off + 64], Sf[:, :], ALU.add, [bs, b_Sf], [b_tmpS])
                    gcol = E1[:, c * 64 + 63:c * 64 + 64]
                    K.ts("dve", Sf[:, :], tmpS[:, :], gcol, ALU.mult, [b_tmpS, b_E1], [b_Sf])
                    K.ts("pool", Sb[:, :], tmpS[:, :], gcol, ALU.mult, [b_tmpS, b_E1], [b_Sb])
                p, bp = big()
                K.mm(p[:, :], bo[:, :], yraw[:, :], [cb, b_yraw], [bp])
                K.stt(ym[:, :], p[:, :], -1.0 / 64, yraw[:, :], ALU.mult, ALU.add, [bp, b_yraw], [b_ym])
                K.act(sqt[:, :], ym[:, :], AF.Square, [b_ym], [b_sqt])
                p, bp = big()
                K.mm(p[:, :], bo[:, :], sqt[:, :], [cb, b_sqt], [bp])
                K.rsqrt_from(rnt[:, :], p[:, :], [bp], [b_rnt], 1.0 / 64, RW_GN_EPS)
                K.tt("dve", ym[:, :], ym[:, :], rnt[:, :], ALU.mult, [b_ym, b_rnt], [b_ym])
                K.ts("dve", ym[:, :], ym[:, :], pc[:, 9:10], ALU.mult, [b_ym, b_pc], [b_ym], s2=pc[:, 10:11], op1=ALU.add)
                K.tt("pool", ym[:, :], ym[:, :], bont[:, :], ALU.add, [b_ym, b_bont], [b_ym])
                K.tt("dve", ys[:, :], ym[:, :], sgt[:, :], ALU.mult, [b_ym, b_sg], [bys])
                S.dma("sp", yo[:, s * 512:(s + 1) * 512], ys[:, :], reads=[bys])
        S.finish()
    return nc


G0 = 0
R0 = 4096
F0 = 4096 + 3200
M0 = F0 + 3088
Q0 = M0 + 2064


def _cols(mixer, c):
    s = lambda a: list(range(a, a + 128))
    if mixer == "rw":
        return s(G0 + c * 128) + s(R0 + c * 128) + s(R0 + 1024 + c * 128) + s(R0 + 2048 + c * 128) + s(R0 + 3072)
    if mixer == "fx":
        return s(G0 + 1024 + c * 128) + s(F0 + c * 128) + s(F0 + 1024 + c * 128) + s(F0 + 2048 + c * 128) + [F0 + 3072 + 2 * c, F0 + 3072 + 2 * c + 1]
    g = c // 2
    return s(G0 + 2048 + c * 128) + s(M0 + c * 128) + s(M0 + 1024 + g * 128) + s(M0 + 1536 + g * 128) + [M0 + 2048 + 2 * c, M0 + 2048 + 2 * c + 1]


def pack_pc(inp, l, c):
    pc = np.zeros((128, NPC), np.float32)
    sl = slice(c * 128, (c + 1) * 128)
    mu = inp["rw_mu"][l]
    pc[:, 0] = mu[0:1024][sl]
    pc[:, 1] = mu[1024:2048][sl]
    pc[:, 2] = mu[2048:3072][sl]
    pc[:, 3] = mu[3072:3200]
    pc[:, 4] = inp["rw_w0"][l][sl]
    pc[:, 5] = inp["rw_a0"][l][sl]
    pc[:, 6] = inp["rw_k_k"][l][sl]
    pc[:, 7] = inp["rw_k_a"][l][sl]
    pc[:, 8] = inp["rw_r_k"][l].reshape(-1)[sl]
    pc[:, 9] = inp["rw_lnx_w"][l][sl]
    pc[:, 10] = inp["rw_lnx_b"][l][sl]
    pc[:, 11] = np.tile(inp["fx_q_norm_w"][l], 2)
    pc[:, 12] = np.tile(inp["fx_k_norm_w"][l], 2)
    cw = inp["ssm_conv_w"][l]
    cbias = inp["ssm_conv_b"][l]
    g = c // 2
    for i, cs in enumerate((slice(c * 128, (c + 1) * 128), slice(1024 + g * 128, 1024 + (g + 1) * 128), slice(1536 + g * 128, 1536 + (g + 1) * 128))):
        for tap in range(4):
            pc[:, 13 + 5 * i + tap] = cw[tap, cs]
        pc[:, 13 + 5 * i + 4] = cbias[cs]
    pc[:, 28] = np.repeat(inp["ssm_d"][l][2 * c:2 * c + 2], 64)
    for h in range(2):
        pc[:, 29 + h] = inp["fx_f_bias"][l][2 * c + h]
        pc[:, 31 + h] = inp["ssm_dt_bias"][l][2 * c + h]
        pc[:, 33 + h] = inp["ssm_a_log"][l][2 * c + h]
    return pc


def pack_p2(inp, l, c, mixer, hT):
    cols = _cols(mixer, c)
    wm = np.ascontiguousarray(inp["w_in"][l][:, cols]).reshape(16, 128, len(cols))
    m = {"hT": hT, "wm": wm, "pc": pack_pc(inp, l, c)}
    if mixer == "rw":
        sl = slice(c * 128, (c + 1) * 128)
        m["lw"] = np.ascontiguousarray(np.concatenate([inp["rw_w_up"][l][:, sl], inp["rw_a_up"][l][:, sl]], axis=0))
    return m


MIXERS_ENABLED = ("rw", "fx", "ssm")
_PROG = {}


def _prog(key, fn):
    if key not in _PROG:
        _PROG[key] = fn()
    return _PROG[key]


def kernel(**inp):
    inp = {k: np.asarray(v) for k, v in inp.items()}
    SEQ = inp["x"].shape[1]
    TOK = SEQ // NCORE
    cores = list(range(NCORE))
    x = np.ascontiguousarray(inp["x"][0]).astype(np.float32)
    xs = [np.ascontiguousarray(x[c * TOK:(c + 1) * TOK]) for c in cores]
    mem = np.ascontiguousarray(inp["mem"][0])
    p1 = _prog(("p1", TOK), lambda: build_p1(TOK))
    p3 = _prog(("p3", TOK), lambda: build_p3(TOK))
    bf = ml_dtypes.bfloat16
    for l in range(DEPTH):
        w_in = inp["w_in"][l]
        qn, kn = inp["mem_q_norm_w"][l], inp["mem_k_norm_w"][l]
        pcm = np.ascontiguousarray(np.stack([qn[:128], qn[128:], kn[:128], kn[128:]], axis=1))
        base = dict(nw=inp["norm_w"][l][None, :], memx=mem, mnw=inp["mem_norm_w"][l][None, :],
                    wkv=inp["mem_w_kv"][l].reshape(16, 128, 2048),
                    wq=np.ascontiguousarray(w_in[:, Q0:Q0 + 1024]).reshape(16, 128, 1024),
                    wg=np.ascontiguousarray(w_in[:, 3072:4096]).reshape(16, 128, 1024), pcm=pcm)
        r1 = run_bass_kernel_spmd(p1, [dict(base, x=xs[c]) for c in cores], core_ids=cores).results
        hT = np.concatenate([r1[c]["hT"] for c in cores], axis=2)
        yT = np.zeros((32, 128, SEQ), bf)
        for gi, mixer in enumerate(("rw", "fx", "ssm")):
            if mixer not in MIXERS_ENABLED:
                continue
            p2 = _prog(("p2", SEQ, mixer), lambda: build_p2(SEQ, mixer))
            r2 = run_bass_kernel_spmd(p2, [pack_p2(inp, l, c, mixer, hT) for c in cores], core_ids=cores).results
            for c in cores:
                yT[gi * 8 + c] = r2[c]["yo"]
        snw = np.ascontiguousarray(inp["ssm_norm_w"][l].reshape(8, 128).T)
        wo = inp["w_out"][l].reshape(32, 128, D_MODEL)
        maps = []
        for c in cores:
            yc = np.ascontiguousarray(yT[:, :, c * TOK:(c + 1) * TOK])
            yc[24:32] = r1[c]["ymT"]
            maps.append(dict(x=xs[c], yT=yc, wo=wo, snw=snw))
        r3 = run_bass_kernel_spmd(p3, maps, core_ids=cores).results
        xs = [r3[c]["xo"] for c in cores]
    return np.concatenate(xs, axis=0)[None].astype(np.float32)
```
